# Optimizing a Trainium2 kernel written in Bass

```python
import math
import jax, jax.numpy as jnp
from jax import lax
import numpy as np

D_MODEL = 1024
BATCH = 4
SEQ = 4096
DEPTH = 2

CHUNK = 64
Q_BLOCK = 128
A_WIDTH = D_MODEL // 2
A_GROUPS = 4
A_GROUP_DIM = A_WIDTH // A_GROUPS
A_BLOCK = 128
B_HEADS = 4
B_HEAD_DIM = 64
B_V_DIM = 2 * B_HEAD_DIM
B_QK = B_HEADS * 2 * B_HEAD_DIM
B_VW = B_HEADS * B_V_DIM
C_WIDTH = D_MODEL // 2
CONV_K = 3
N_BRANCH = 3
D_FF = 4 * D_MODEL
SPLIT_SIZES = [A_WIDTH, A_WIDTH, B_QK, B_QK, B_VW, C_WIDTH, C_WIDTH, C_WIDTH, N_BRANCH * D_MODEL]
N_IN = int(sum(SPLIT_SIZES))
SPLIT_IDX = [int(v) for v in np.cumsum(SPLIT_SIZES)[:-1]]
EPS = 1e-6

kernel_name = "hybrid_gmlp_diffattn_shortconv_block"


def rms_norm(x, w):
    xf = x.astype(jnp.float32)
    y = xf * lax.rsqrt(jnp.mean(xf * xf, axis=-1, keepdims=True) + EPS)
    return (y * w.astype(jnp.float32)).astype(x.dtype)


def alibi_slopes(n_heads):
    return 2.0 ** (-8.0 * jnp.arange(1, n_heads + 1, dtype=jnp.float32) / n_heads)


def gmlp_spatial_gating(a_u, a_v, vnorm_w, ws, bs):
    b, s, _ = a_u.shape
    u = jax.nn.gelu(a_u)
    v = rms_norm(jax.nn.gelu(a_v), vnorm_w)
    vb = v.reshape(b, s // A_BLOCK, A_BLOCK, A_GROUPS, A_GROUP_DIM)
    causal = jnp.tril(jnp.ones((A_BLOCK, A_BLOCK), dtype=ws.dtype))
    ws_c = ws * causal[None]
    sv = jnp.einsum('gts,bnsgc->bntgc', ws_c, vb) + bs.T[:, :, None]
    return u * sv.reshape(b, s, A_WIDTH)


def differential_attention(q, k, v, lam, slopes):
    s_len = q.shape[1]
    scale = B_HEAD_DIM ** -0.5
    outs = []
    for i in range(s_len // Q_BLOCK):
        kv_len = (i + 1) * Q_BLOCK
        q_blk = q[:, i * Q_BLOCK:kv_len]
        k_blk = k[:, :kv_len]
        v_blk = v[:, :kv_len]
        sc = jnp.einsum('bqhmd,bkhmd->bhmqk', q_blk, k_blk).astype(jnp.float32) * scale
        qpos = i * Q_BLOCK + jnp.arange(Q_BLOCK)
        kpos = jnp.arange(kv_len)
        dist = jnp.abs(qpos[:, None] - kpos[None, :]).astype(jnp.float32)
        bias = -slopes[:, None, None, None] * dist
        mask = (kpos[None, :] // CHUNK) <= (qpos[:, None] // CHUNK)
        p = jax.nn.softmax(jnp.where(mask, sc + bias, -jnp.inf), axis=-1)
        a = p[:, :, 0] - lam * p[:, :, 1]
        outs.append(jnp.einsum('bhqk,bkhe->bqhe', a.astype(v.dtype), v_blk))
    return jnp.concatenate(outs, axis=1)


def causal_depthwise_conv(z, w):
    kern = w.astype(z.dtype)[:, None, :]
    return lax.conv_general_dilated(z, kern, window_strides=(1,), padding=[(CONV_K - 1, 0)],
                                    dimension_numbers=('NWC', 'WIO', 'NWC'),
                                    feature_group_count=z.shape[-1])


def setup_inputs(seed: int = 0) -> dict:
    key = jax.random.key(seed)
    ks = jax.random.split(key, 24)
    L, D = DEPTH, D_MODEL
    nrm = lambda k, shape, sc: jax.random.normal(k, shape, jnp.float32) * sc
    tri = jnp.tril(jnp.ones((A_BLOCK, A_BLOCK), jnp.float32))
    return {
        "x": nrm(ks[0], (BATCH, SEQ, D), 1.0),
        "norm1_w": 1.0 + nrm(ks[1], (L, D), 0.02),
        "w_in": nrm(ks[2], (L, D, N_IN), D ** -0.5),
        "gate_b": nrm(ks[3], (L, N_BRANCH, D), 0.02),
        "a_vnorm_w": 1.0 + nrm(ks[4], (L, A_WIDTH), 0.02),
        "a_ws": nrm(ks[5], (L, A_GROUPS, A_BLOCK, A_BLOCK), 0.5 * A_BLOCK ** -0.5) * tri,
        "a_bs": 1.0 + nrm(ks[6], (L, A_GROUPS, A_BLOCK), 0.02),
        "b_qnorm_w": 1.0 + nrm(ks[7], (L, 2, B_HEAD_DIM), 0.02),
        "b_knorm_w": 1.0 + nrm(ks[8], (L, 2, B_HEAD_DIM), 0.02),
        "b_lam": nrm(ks[9], (L, 4, B_HEAD_DIM), 0.1),
        "b_subnorm_w": 1.0 + nrm(ks[10], (L, B_V_DIM), 0.02),
        "c_conv_w": nrm(ks[11], (L, CONV_K, C_WIDTH), CONV_K ** -0.5),
        "w_br_a": nrm(ks[12], (L, A_WIDTH, D), A_WIDTH ** -0.5),
        "w_br_b": nrm(ks[13], (L, B_VW, D), B_VW ** -0.5),
        "w_br_c": nrm(ks[14], (L, C_WIDTH, D), C_WIDTH ** -0.5),
        "w_out": nrm(ks[15], (L, D, D), D ** -0.5),
        "norm2_w": 1.0 + nrm(ks[16], (L, D), 0.02),
        "w_ff1": nrm(ks[17], (L, D, D_FF), D ** -0.5),
        "w_ff2": nrm(ks[18], (L, D_FF, D), D_FF ** -0.5),
    }


def reference(x, norm1_w, w_in, gate_b, a_vnorm_w, a_ws, a_bs, b_qnorm_w, b_knorm_w, b_lam,
              b_subnorm_w, c_conv_w, w_br_a, w_br_b, w_br_c, w_out, norm2_w, w_ff1, w_ff2):
    b, s, d = x.shape
    slopes = alibi_slopes(B_HEADS)
    for l in range(DEPTH):
        h = rms_norm(x, norm1_w[l])
        proj = h @ w_in[l]
        a_u, a_v, q, k, v, c_b, c_c, c_x, g_pre = jnp.split(proj, SPLIT_IDX, axis=-1)

        y_a = gmlp_spatial_gating(a_u, a_v, a_vnorm_w[l], a_ws[l], a_bs[l])

        q = rms_norm(q.reshape(b, s, B_HEADS, 2, B_HEAD_DIM), b_qnorm_w[l])
        k = rms_norm(k.reshape(b, s, B_HEADS, 2, B_HEAD_DIM), b_knorm_w[l])
        v = v.reshape(b, s, B_HEADS, B_V_DIM)
        lam_p = b_lam[l].astype(jnp.float32)
        lambda_init = 0.8 - 0.6 * math.exp(-0.3 * l)
        lam = (jnp.exp(jnp.sum(lam_p[0] * lam_p[1])) - jnp.exp(jnp.sum(lam_p[2] * lam_p[3]))
               + lambda_init)
        o = differential_attention(q, k, v, lam, slopes)
        o = rms_norm(o, b_subnorm_w[l]) * (1.0 - lambda_init)
        y_b = o.reshape(b, s, B_VW)

        y_c = c_b * causal_depthwise_conv(c_c * c_x, c_conv_w[l])

        gates = jax.nn.sigmoid((g_pre.reshape(b, s, N_BRANCH, d) + gate_b[l]).astype(jnp.float32)).astype(x.dtype)
        merged = (gates[:, :, 0] * (y_a @ w_br_a[l])
                  + gates[:, :, 1] * (y_b @ w_br_b[l])
                  + gates[:, :, 2] * (y_c @ w_br_c[l]))
        x = x + merged @ w_out[l]

        h2 = rms_norm(x, norm2_w[l])
        x = x + jnp.square(jax.nn.relu(h2 @ w_ff1[l])) @ w_ff2[l]
    return x
```

```python
import math
import os
from contextlib import ExitStack

import numpy as np
import ml_dtypes
import concourse.bass as bass
import concourse.mybir as mybir
from concourse.bass_utils import run_bass_kernel_spmd

F32 = mybir.dt.float32
BF = mybir.dt.bfloat16
AF = mybir.ActivationFunctionType
ALU = mybir.AluOpType

T = 2048
D = 1024
NTT = 4
EPS = 1e-6
NEG = -30000.0
SLOPES = [2.0 ** (-8.0 * (h + 1) / 4) for h in range(4)]
FKV = 2048 + 16 * 129 + 16
VOFF = 2048
ZOFF = 2048 + 2064
NDMA = 48
KB = 1024

PC_N1, PC_N2, PC_GB, PC_QN, PC_KN, PC_CW, PC_SUB, PC_L = 0, 8, 16, 40, 41, 42, 54, 55
PB_VN, PB_SUB, PB_LAM, PB_BS, PB_L = 0, 512, 640, 896, 1408
CF_ID, CF_TRI, CF_BOWN, CF_BPRE, CF_HF, CF_N = 0, 128, 256, 320, 432, 436
CB_ONES, CB_BLK, CB_ID, CB_CORR, CB_N = 0, 128, 256, 384, 896


class Sched:
    def __init__(self):
        self.ops = {e: [] for e in ("pe", "act", "dve", "pool", "sp")}
        self.cnt = {e: 0 for e in self.ops}
        self.waited = {e: {} for e in self.ops}
        self.lastw = {}
        self.readers = {}
        self.dma_next_q = {"sp": 0, "pool": 0}
        self.dma_cnt = [0] * NDMA

    @staticmethod
    def keys(ap):
        if isinstance(ap, tuple):
            return [ap]
        dsz = 4 if ap.dtype == F32 else 2
        dims = ap.ap
        pstride = dims[0][0]
        off = ap.offset % pstride if pstride else ap.offset
        hi = off + sum((c - 1) * st for st, c in dims[1:]) + 1
        lo_b, hi_b = off * dsz, hi * dsz
        if str(ap.space) == "PSUM":
            return [("p", ap.tensor.name)]
        base = ("s",)
        return [base + (b,) for b in range(lo_b // 256, (hi_b - 1) // 256 + 1)]

    def op(self, eng, fn, reads=(), writes=(), inc=True, dma=False, sem=None):
        inc = True
        rk = [k for a in reads if a is not None for k in self.keys(a)]
        wk = [k for a in writes if a is not None for k in self.keys(a)]
        deps = {}

        def add(s, v):
            if deps.get(s, 0) < v:
                deps[s] = v

        for k in rk:
            t = self.lastw.get(k)
            if t:
                add(*t)
            if k[0] == "p":
                for s, v in self.readers.get(k, {}).items():
                    if s != eng:
                        add(s, v)
        for k in wk:
            t = self.lastw.get(k)
            if t:
                add(*t)
            for s, v in self.readers.get(k, {}).items():
                add(s, v)
        if eng == "pe":
            deps.pop("pe", None)
        waits = []
        wd = self.waited[eng]
        for s, v in deps.items():
            if wd.get(s, 0) < v:
                waits.append((s, v))
                wd[s] = v
        if dma:
            half = NDMA // 2
            base = 0 if eng == "sp" else half
            j = base + self.dma_next_q[eng]
            self.dma_next_q[eng] = (self.dma_next_q[eng] + 1) % half
            sn = ("dma", j)
            prev = self.dma_cnt[j]
            if prev and wd.get(sn, 0) < prev:
                waits.append((sn, prev))
                wd[sn] = prev
            self.dma_cnt[j] += 16
            tok = (sn, self.dma_cnt[j])
            incspec = (sn, 16)
        elif sem is not None:
            tok = (sem, 1)
            incspec = (sem, 1)
        elif inc:
            self.cnt[eng] += 1
            tok = (eng, self.cnt[eng])
            incspec = (eng, 1)
        else:
            tok = (eng, self.cnt[eng] + 1)
            incspec = None
        self.ops[eng].append((waits, fn, incspec))
        for k in wk:
            self.lastw[k] = tok
            self.readers[k] = {}
        for k in rk:
            r = self.readers.setdefault(k, {})
            if r.get(tok[0], 0) < tok[1]:
                r[tok[0]] = tok[1]
        return tok


def build(layers, first, last, dbg=None):
    nc = bass.Bass("TRN2", target_bir_lowering=False)
    S = Sched()
    x_d = nc.dram_tensor("x", [T, D], F32, kind="ExternalInput").ap()
    out_d = nc.dram_tensor("out", [T, D], F32, kind="ExternalOutput").ap()
    LIGHT = bool(os.environ.get("K_LIGHT"))
    wshape = (lambda s_: [2, 8, 8]) if LIGHT else (lambda s_: s_)
    w_in_d = nc.dram_tensor("w_in", wshape([2, D, 7168]), F32, kind="ExternalInput").ap()
    w_br_d = nc.dram_tensor("w_br", wshape([2, 3, 512, D]), F32, kind="ExternalInput").ap()
    w_out_d = nc.dram_tensor("w_out", wshape([2, D, D]), F32, kind="ExternalInput").ap()
    w_ff1_d = nc.dram_tensor("w_ff1", wshape([2, D, 4096]), F32, kind="ExternalInput").ap()
    w_ff2_d = nc.dram_tensor("w_ff2", wshape([2, 4096, D]), F32, kind="ExternalInput").ap()
    if LIGHT:
        class _Dummy:
            def __getitem__(self, k):
                return self
        w_in_d = w_br_d = w_out_d = w_ff1_d = w_ff2_d = _Dummy()
    wsT_d = nc.dram_tensor("wsT", [2, 128, 4, 128], F32, kind="ExternalInput").ap()
    pcol_d = nc.dram_tensor("pcol", [128, 2 * PC_L], F32, kind="ExternalInput").ap()
    pbc_d = nc.dram_tensor("pbc", [128, 2 * PB_L], F32, kind="ExternalInput").ap()
    cstf_d = nc.dram_tensor("cstf", [128, CF_N], F32, kind="ExternalInput").ap()
    cstb_d = nc.dram_tensor("cstb", [128, CB_N], BF, kind="ExternalInput").ap()
    snd = {(l, h): nc.dram_tensor(f"snd{l}_{h}", [128, FKV], BF) for l in layers for h in range(4)}
    rcv = {(l, h): nc.dram_tensor(f"rcv{l}_{h}", [256, FKV], BF) for l in layers for h in range(4)}
    dbg_d = None
    if dbg:
        dbg_d = nc.dram_tensor("dbg", [128, dbg[1]], F32, kind="ExternalOutput").ap()

    es = ExitStack()
    ARENA_B = 206 * KB
    arena = es.enter_context(nc.sbuf_tensor("arena", [128, ARENA_B // 4], F32))
    psb = [es.enter_context(nc.psum_tensor(f"ps{i}", [128, 512], F32)) for i in range(8)]
    sems = {e: es.enter_context(nc.semaphore(f"s_{e}")) for e in ("pe", "act", "dve", "pool", "sp")}
    for j in range(NDMA):
        sems[("dma", j)] = es.enter_context(nc.semaphore(f"s_dma{j}"))
    for l in layers:
        for h in range(4):
            sems[("cc", l, h)] = es.enter_context(nc.semaphore(f"s_cc{l}_{h}"))

    def V(off, n, dt=F32):
        assert off % 4 == 0 and off + n * (4 if dt == F32 else 2) <= ARENA_B, (off, n)
        if dt == F32:
            return arena[:, off // 4: off // 4 + n]
        assert n % 2 == 0
        return arena[:, off // 4: off // 4 + n // 2].bitcast(BF)

    def r3(ap, a):
        return ap.rearrange("p (a b) -> p a b", a=a)

    XT = r3(V(0, 8 * T), 8)
    RING0 = 64 * KB
    C0 = 96 * KB
    cstf = V(C0, CF_N)
    cstb = V(C0 + 1792, CB_N, BF)
    pcol = V(C0 + 3584, 2 * PC_L)
    pbc = V(C0 + 4096, 2 * PB_L)
    o = C0 + 4096 + 2 * PB_L * 4
    wsTm = r3(V(o, 2 * 512, BF), 2); o += 2048
    lamcol = V(o, 8); o += 32
    zcol = V(o, 8); o += 32
    ones_f = V(o, 128); o += 512
    assert o <= 114 * KB
    B0 = 114 * KB
    BSZ = ARENA_B - B0

    def VB(off, n, dt=F32):
        assert off + n * (4 if dt == F32 else 2) <= BSZ, (off, n, dt)
        return V(B0 + off, n, dt)

    ident_f = cstf[:, CF_ID:CF_ID + 128]
    tri_f = cstf[:, CF_TRI:CF_TRI + 128]
    hflag = cstf[:, CF_HF:CF_HF + 1]
    ones_b = cstb[:, CB_ONES:CB_ONES + 128]
    blk_b = cstb[:, CB_BLK:CB_BLK + 128]
    ident_b = cstb[:, CB_ID:CB_ID + 128]
    corr_b = r3(cstb[:, CB_CORR:CB_CORR + 512], 4)

    def mm(out, lhsT, rhs, start=True, stop=True, inc=None, skip=False):
        if inc is None:
            inc = stop
        S.op("pe", lambda e: e.matmul(out, lhsT, rhs, start=start, stop=stop, skip_group_check=skip),
             reads=[lhsT, rhs], writes=[out], inc=inc)

    def tr(out, in_, inc=True):
        S.op("pe", lambda e: e.transpose(out, in_, ident_f), reads=[in_, ident_f], writes=[out], inc=inc)

    def act(out, in_, func, bias=None, scale=None, accum_out=None):
        kw = {}
        rd = [in_]
        if bias is not None:
            kw["bias"] = bias
            if not isinstance(bias, float):
                rd.append(bias)
        if scale is not None:
            kw["scale"] = scale
        wr = [out]
        if accum_out is not None:
            kw["accum_out"] = accum_out
            wr.append(accum_out)
        S.op("act", lambda e: e.activation(out, in_, func, **kw), reads=rd, writes=wr)

    def tt(out, in0, in1, op, eng="dve"):
        S.op(eng, lambda e: e.tensor_tensor(out, in0, in1, op), reads=[in0, in1], writes=[out])

    def ts(out, in0, s1, op0, s2=None, op1=None, eng="dve"):
        rd = [in0] + [s for s in (s1, s2) if s is not None and not isinstance(s, float)]
        if op1 is None:
            S.op(eng, lambda e: e.tensor_scalar(out, in0, s1, None, op0), reads=rd, writes=[out])
        else:
            S.op(eng, lambda e: e.tensor_scalar(out, in0, s1, s2, op0, op1), reads=rd, writes=[out])

    def stt(out, in0, scalar, in1, op0, op1):
        rd = [in0, in1] + ([] if isinstance(scalar, float) else [scalar])
        S.op("dve", lambda e: e.scalar_tensor_tensor(out, in0, scalar, in1, op0, op1), reads=rd, writes=[out])

    def recip(out, in_):
        S.op("dve", lambda e: e.reciprocal(out, in_), reads=[in_], writes=[out])

    def memset(ap, val):
        S.op("dve", lambda e: e.memset(ap, val), writes=[ap])

    def dma(q, out, in_, reads=(), writes=()):
        return S.op(q, lambda e: e.dma_start(out=out, in_=in_), reads=list(reads), writes=list(writes), dma=True)

    bank_ctr = [0]

    def bank():
        b = psb[bank_ctr[0] % 8]
        bank_ctr[0] += 1
        return b

    ring_ctr = [0]

    def slab(src, kc, ncol):
        assert kc * ncol * 2 <= 8 * KB
        slot = ring_ctr[0] % 4
        ring_ctr[0] += 1
        dst = r3(V(RING0 + slot * 8 * KB, kc * ncol, BF), kc)
        srcv = None if LIGHT else src.rearrange("(k p) n -> p k n", p=128)
        for k_ in range(kc):
            if LIGHT:
                continue
            dma("pool", dst[:, k_, :], srcv[:, k_, :], writes=[dst[:, k_, :]])
        return dst

    def tsl(t):
        return slice(t * 512, (t + 1) * 512)

    def dump(ap, ncols):
        if dbg_d is not None:
            dma("sp", dbg_d[:, 0:ncols], ap, reads=[ap], writes=[("dram", "dbg")])

    def rsqrt_ps(out, ps_in, scale):
        act(out, ps_in, AF.Sqrt, bias=epscol, scale=scale)
        recip(out, out)

    epscol = zcol[:, 0:1]

    dma("sp", cstf, cstf_d, writes=[cstf])
    dma("sp", cstb, cstb_d, writes=[cstb])
    dma("sp", pcol, pcol_d, writes=[pcol])
    dma("sp", pbc, pbc_d, writes=[pbc])
    memset(zcol[:, 0:1], EPS)
    memset(ones_f, 1.0)

    def gelu_from_psum(ps, out, t1):
        act(out, ps, AF.Gelu_apprx_tanh)

    if first:
        for tb in range(16):
            xin = VB(56 * KB + (tb % 4) * 4 * KB, 1024)
            dma("sp", xin, x_d[tb * 128:(tb + 1) * 128, :], writes=[xin])
            for c4 in range(2):
                ps = bank()
                for j in range(4):
                    c = c4 * 4 + j
                    tr(ps[:, j * 128:(j + 1) * 128], xin[:, c * 128:(c + 1) * 128], inc=(j == 3))
                act(XT[:, c4 * 4:(c4 + 1) * 4, tb * 128:(tb + 1) * 128], r3(ps[:, :], 4), AF.Copy)
    else:
        raise NotImplementedError

    STOP = int(os.environ.get('K_STOP', '99'))
    for l in layers:
        lam_init = 0.8 - 0.6 * math.exp(-0.3 * l)
        pc = lambda o_, n=1: pcol[:, l * PC_L + o_: l * PC_L + o_ + n]
        pb = lambda o_, n: pbc[:, l * PB_L + o_: l * PB_L + o_ + n]
        w_in_l = w_in_d[l]

        lam = lamcol[:, 4 * l:4 * l + 1]
        lt = VB(56 * KB, 128)
        if STOP < -3:
            continue
        tt(lt[:, 0:64], pb(PB_LAM, 64), pb(PB_LAM + 64, 64), ALU.mult)
        tt(lt[:, 64:128], pb(PB_LAM + 128, 64), pb(PB_LAM + 192, 64), ALU.mult)
        S.op("dve", lambda e, a=lamcol[:, 4 * l + 1:4 * l + 3], b=r3(lt, 2): e.tensor_reduce(
            a, b, mybir.AxisListType.X, ALU.add), reads=[lt], writes=[lamcol[:, 4 * l + 1:4 * l + 3]])
        act(lamcol[:, 4 * l + 1:4 * l + 3], lamcol[:, 4 * l + 1:4 * l + 3], AF.Exp)
        tt(lam, lamcol[:, 4 * l + 1:4 * l + 2], lamcol[:, 4 * l + 2:4 * l + 3], ALU.subtract)
        ts(lam, lam, lam_init, ALU.add)
        ts(pb(PB_SUB, 128), pb(PB_SUB, 128), 1.0 - lam_init, ALU.mult)
        ts(pc(PC_SUB), pc(PC_SUB), 1.0 - lam_init, ALU.mult)
        if STOP < -2:
            continue
        wst = VB(56 * KB + 1024, 512)
        dma("sp", wst, wsT_d[l].rearrange("s g t -> s (g t)"), writes=[wst])
        for g in range(4):
            tt(wsTm[:, l, g * 128:(g + 1) * 128], wst[:, g * 128:(g + 1) * 128], tri_f, ALU.mult)

        hT = r3(VB(0, 8 * T, BF), 8)
        rstd = VB(32 * KB, T)
        yc = r3(VB(40 * KB, 4 * T, BF), 4)

        def norm_stats(sqoff):
            for t in range(NTT):
                ps = bank()
                for c in range(8):
                    sq = VB(sqoff + (c % 2) * KB, 512, BF)
                    act(sq, XT[:, c, tsl(t)], AF.Square)
                    mm(ps[:, :], ones_b, sq, start=(c == 0), stop=(c == 7))
                rsqrt_ps(rstd[:, tsl(t)], ps[:, :], 1.0 / D)

        def norm_apply(dst, wo, t0, n):
            for c in range(8):
                stt(dst[:, c, 0:n], XT[:, c, t0:t0 + n], pc(wo + c), rstd[:, t0:t0 + n], ALU.mult, ALU.mult)

        if STOP < -1:
            continue
        norm_stats(88 * KB)
        norm_apply(hT, PC_N1, 0, T)
        if dbg and dbg[0] == f"hT{l}":
            tmpd = VB(56 * KB, 2048)
            act(tmpd, hT[:, 3, :], AF.Copy)
            dump(tmpd, 2048)

        def dense_ws(sl, col0, kcn, rhs_of, tts, evac):
            for t in tts:
                ps = bank()
                for kc in range(kcn):
                    mm(ps[:, :], sl[:, kc, col0:col0 + 128], rhs_of(kc, t), start=(kc == 0), stop=(kc == kcn - 1))
                evac(t, ps)

        hT_of = lambda kc, t: hT[:, kc, tsl(t)]

        if STOP < 1:
            continue
        sl_cc = slab(w_in_l[:, 3072:3584], 8, 512)
        sl_cx = slab(w_in_l[:, 3584:4096], 8, 512)
        sl_cb = slab(w_in_l[:, 2560:3072], 8, 512)
        zbuf = VB(56 * KB, 2052)
        accb = VB(56 * KB + 8208, 2048)
        smallb = VB(73984, 64)
        cbtmp = VB(78 * KB, 64)
        cbh = r3(smallb[:, 0:8], 4)
        ycp = r3(smallb[:, 8:16], 4)
        Fh = r3(smallb[:, 16:24], 4)
        zh = r3(smallb[:, 24:32], 4)
        tA = smallb[:, 32:36]
        tB = smallb[:, 36:40]
        tC = r3(smallb[:, 40:48], 4)
        zsend16 = VB(73984 + 256, 16, BF)
        zsend = r3(zsend16[:, 0:8], 4)
        zhraw = r3(VB(73984 + 256 + 32, 8, BF), 4)
        memset(zsend16, 0.0)
        cct = [VB(74 * KB + i * 2 * KB, 512) for i in range(2)]
        memset(zbuf[:, 0:2], 0.0)
        cc_ctr = [0]
        for i in range(4):
            def ev_cc(t, ps, i=i):
                c = cct[cc_ctr[0] % 2]
                cc_ctr[0] += 1
                act(c, ps[:, :], AF.Copy)
                psB = bank()
                for kc in range(8):
                    mm(psB[:, :], sl_cx[:, kc, i * 128:(i + 1) * 128], hT_of(kc, t), start=(kc == 0), stop=(kc == 7))
                tt(zbuf[:, 2 + t * 512: 2 + (t + 1) * 512], c, psB[:, :], ALU.mult)
            dense_ws(sl_cc, i * 128, 8, hT_of, range(NTT), ev_cc)
            P1L = int(os.environ.get("K_P1", "9"))
            if P1L < 2:
                continue
            w0, w1, w2 = pc(PC_CW + 0 * 4 + i), pc(PC_CW + 1 * 4 + i), pc(PC_CW + 2 * 4 + i)
            ts(accb, zbuf[:, 2:2050], w2, ALU.mult)
            if P1L < 3:
                continue
            stt(accb, zbuf[:, 1:2049], w1, accb, ALU.mult, ALU.add)
            stt(accb, zbuf[:, 0:2048], w0, accb, ALU.mult, ALU.add)
            if P1L < 4:
                continue
            act(zsend[:, i, :], zbuf[:, 2048:2050], AF.Copy)
            if P1L < 5:
                continue

            def ev_cb(t, ps, i=i):
                tt(yc[:, i, tsl(t)], accb[:, tsl(t)], ps[:, :], ALU.mult)
                TINY = int(os.environ.get('K_TINY', '9'))
                if t == 0 and TINY >= 1:
                    if os.environ.get('K_CBACT'):
                        act(cbtmp, ps[:, 0:64], AF.Copy)
                    else:
                        ts(cbtmp, ps[:, 0:64], 1.0, ALU.mult)
                    if TINY >= 2:
                        act(cbh[:, i, :], cbtmp[:, 0:2], AF.Copy)
                    if TINY >= 3:
                        tt(ycp[:, i, :], cbtmp[:, 0:2], accb[:, 0:2], ALU.mult)
            dense_ws(sl_cb, i * 128, 8, hT_of, range(NTT), ev_cb)

        if STOP < 2:
            continue
        qk_pending = []

        def qk_flush():
            while qk_pending:
                qk_pending.pop(0)()

        def qknorm_evac(dst_of, wcol, tmpoff):
            def ev(t, ps, h):
                qk_flush()
                sq = VB(tmpoff, 512, BF)
                act(sq, ps[:, :], AF.Square)

                def stage2():
                    ps2 = bank()
                    mm(ps2[:, :], blk_b, sq)
                    rt = VB(tmpoff + KB, 512)
                    rsqrt_ps(rt, ps2[:, :], 1.0 / 64)
                    stt(dst_of(h, t), ps[:, :], wcol, rt, ALU.mult, ALU.mult)
                stage2()
            return ev

        sl_k = slab(w_in_l[:, 1536:2048], 8, 512)
        sl_v = slab(w_in_l[:, 2048:2560], 8, 512)
        kst = [VB(56 * KB + i * 4 * KB, 2048, BF) for i in range(2)]
        vst = [VB(64 * KB + i * 4224, 4 * 4 * 129, BF).rearrange("p (h t e) -> p h t e", h=4, t=4) for i in range(2)]
        for i in range(2):
            memset(vst[i][:, :, :, 128:129], 1.0)
        snd_ap = [snd[(l, h)].ap() for h in range(4)]
        rcv_ap = [rcv[(l, h)].ap() for h in range(4)]
        kev = qknorm_evac(lambda h, t: kst[h % 2][:, tsl(t)], pc(PC_KN), 73 * KB)
        for h in range(4):
            dense_ws(sl_k, h * 128, 8, hT_of, range(NTT), lambda t, ps, h=h: kev(t, ps, h))
            qk_flush()
            dma("sp", snd_ap[h][:, 0:2048], kst[h % 2], reads=[kst[h % 2]],
                writes=[("dram", "snd", l, "k", h)])
        sl_q = slab(w_in_l[:, 1024:1536], 8, 512)
        vdst = [snd_ap[h][:, VOFF:VOFF + 2064].rearrange("p (t e) -> p t e", t=16) for h in range(4)]
        for tb in range(16):
            ps = bank()
            for kc in range(8):
                mm(ps[:, :], hT[:, kc, tb * 128:(tb + 1) * 128], sl_v[:, kc, :], start=(kc == 0), stop=(kc == 7))
            vq = vst[(tb // 4) % 2]
            act(vq[:, :, tb % 4, 0:128], r3(ps[:, :], 4), AF.Copy)
            if tb % 4 == 3:
                q4 = tb // 4
                for h in range(4):
                    dma("sp", vdst[h][:, q4 * 4:(q4 + 1) * 4, :], vq[:, h, :, :], reads=[vq],
                        writes=[("dram", "snd", l, "v", h, q4)])
        dma("sp", snd_ap[0][:, ZOFF:ZOFF + 16], zsend16, reads=[zsend16], writes=[("dram", "snd", l, "z")])
        for h in range(1, 4):
            dma("sp", snd_ap[h][:, ZOFF:ZOFF + 16], zsend16, reads=[zsend16], writes=[("dram", "snd", l, "z", h)])
        for h in range(4):
            allsnd = [("dram", "snd", l, "k", h)] + [("dram", "snd", l, "v", h, q) for q in range(4)]
            allsnd.append(("dram", "snd", l, "z") if h == 0 else ("dram", "snd", l, "z", h))
            S.op("pool", lambda e, a=snd[(l, h)], b=rcv[(l, h)]: e.collective_compute(
                "AllGather", ALU.bypass, replica_groups=[[0, 1], [2, 3], [4, 5], [6, 7]],
                ins=[a.ap().opt()], outs=[b.ap().opt()]), reads=allsnd, writes=[("dram", "rcv", l, h)],
                sem=("cc", l, h))

        if STOP < 3:
            continue
        QT = r3(VB(76 * KB, 4 * T, BF), 4)
        qev = qknorm_evac(lambda h, t: QT[:, h, tsl(t)], pc(PC_QN), 73 * KB)
        for h in range(4):
            dense_ws(sl_q, h * 128, 8, hT_of, range(NTT), lambda t, ps, h=h: qev(t, ps, h))
        qk_flush()

        if STOP < 5:
            continue
        kown = VB(0, 2048, BF)
        kpre = VB(4 * KB, 2048, BF)
        vown = r3(VB(8 * KB, 16 * 129, BF), 16)
        vpre = r3(VB(8 * KB + 4128, 16 * 129, BF), 16)
        Et = [[VB(17 * KB + (i * 2 + m) * KB, 512, BF) for m in range(2)] for i in range(3)]
        ev_rz = VB(23 * KB, 2)
        ev_ss = VB(23 * KB + 16, 2)
        ev_tmp = VB(23 * KB + 256, 128)
        ev_o = VB(23 * KB + 256 + 512, 128)
        ev_y = [VB(24 * KB + 256 + i * 512, 128) for i in range(4)]
        ybT = r3(VB(56 * KB, 4 * T, BF), 4)
        OT = [psb[4 + m][:, :] for m in range(2)]
        ev_rs = VB(73 * KB, 512)
        pending = []
        pending1 = []
        Zacc = [VB(23 * KB + m * 2 * KB, 512) for m in range(2)]
        rz = [VB(27 * KB + m * 2 * KB, 512) for m in range(2)]
        bown = cstf[:, CF_BOWN:CF_BOWN + 64]
        bpre = cstf[:, CF_BPRE:CF_BPRE + 112]
        for h in range(4):
            dma("sp", kown, snd_ap[h][:, 0:2048], reads=[("dram", "snd", l, "k", h)], writes=[kown])
            vsrc = snd_ap[h][:, VOFF:VOFF + 2064].rearrange("p (t e) -> p t e", t=16)
            dma("sp", vown, vsrc, reads=[("dram", "snd", l, "v", h, q) for q in range(4)], writes=[vown])
            dma("sp", kpre, rcv_ap[h][0:128, 0:2048], reads=[("dram", "rcv", l, h)], writes=[kpre])
            vsrc2 = rcv_ap[h][0:128, VOFF:VOFF + 2064].rearrange("p (t e) -> p t e", t=16)
            dma("sp", vpre, vsrc2, reads=[("dram", "rcv", l, h)], writes=[vpre])
            for qt in range(4):
                steps = [("d", j) for j in range(4)] + [("o", kt) for kt in range(4 * qt)] + [("p", kt) for kt in range(16)]
                ns = len(steps)

                def qk(i):
                    kind, a = steps[i]
                    for m in range(2):
                        Sb = psb[2 * (i % 2) + m]
                        rows = slice(m * 64, (m + 1) * 64)
                        if kind == "d":
                            kt = 4 * qt + a
                            ks = kown[rows, kt * 128:(kt + 1) * 128]
                            q0 = qt * 512 + a * 128
                            mm(Sb[:, a * 128:(a + 1) * 128], ks, QT[rows, h, q0:q0 + 128], start=True, stop=False, inc=False)
                            mm(Sb[:, a * 128:(a + 1) * 128], ident_b, corr_b[:, h, :], start=False, stop=True, inc=(a == 3))
                            if a < 3:
                                mm(Sb[:, (a + 1) * 128:512], ks, QT[rows, h, q0 + 128:(qt + 1) * 512])
                        else:
                            kb = kown if kind == "o" else kpre
                            mm(Sb[:, :], kb[rows, a * 128:(a + 1) * 128], QT[rows, h, tsl(qt)])

                def ex(i):
                    kind, a = steps[i]
                    if kind == "p":
                        dlt = 128 * a - 2048 - 512 * qt
                        bc = bpre[:, h * 28 + (dlt + 3584) // 128: h * 28 + (dlt + 3584) // 128 + 1]
                    else:
                        kt = 4 * qt + a if kind == "d" else a
                        dlt = 128 * kt - 512 * qt
                        bc = bown[:, h * 16 + (dlt + 1536) // 128: h * 16 + (dlt + 1536) // 128 + 1]
                    c0 = a * 128 if kind == "d" else 0
                    for m in range(2):
                        act(Et[i % 3][m][:, c0:512], psb[2 * (i % 2) + m][:, c0:512], AF.Exp, bias=bc, scale=0.125)

                def av(i):
                    kind, a = steps[i]
                    c0 = a * 128 if kind == "d" else 0
                    if kind == "d":
                        vt = vown[:, 4 * qt + a, 0:128]
                    elif kind == "o":
                        vt = vown[:, a, 0:128]
                    else:
                        vt = vpre[:, a, 0:128]
                    for m in range(2):
                        mm(OT[m][:, c0:512], vt, Et[i % 3][m][:, c0:512], start=(i == 0), stop=(i == ns - 1), skip=True)
                        if m == 1 and i >= 8 and i % 2 == 0:
                            mm(psb[7][:, :], ones_b, Et[i % 3][m], start=(i == 8), stop=False, skip=True)
                        elif i == 0:
                            ts(Zacc[m], Et[i % 3][m], 1.0, ALU.mult)
                        else:
                            tt(Zacc[m][:, c0:512], Zacc[m][:, c0:512], Et[i % 3][m][:, c0:512], ALU.add)

                def filler(i):
                    if i >= 1:
                        for s in range(4):
                            mm(psb[4 + s][:, 258:512], ident_b, cstb[:, CB_CORR:CB_CORR + 254],
                               start=False, stop=False, skip=True)

                qk(0)
                for i in range(ns):
                    ex(i)
                    if i + 1 < ns:
                        qk(i + 1)
                    if i == 0 and pending1:
                        pending1.pop()()
                    av(i)
                    if i == 2 and pending:
                        pending.pop()()
                for m in range(2):
                    act(rz[m], OT[m], AF.Copy)

                def evac1():
                    mm(psb[6][:, :], ones_f, Zacc[0])
                    mm(psb[7][:, :], ones_f, Zacc[1], start=False, stop=True, skip=True)
                    for m in range(2):
                        recip(psb[6 + m][:, :], psb[6 + m][:, :])
                        tt(rz[m], rz[m], psb[6 + m][:, :], ALU.mult)
                    ts(rz[1], rz[1], lam, ALU.mult)
                    tt(rz[0], rz[0], rz[1], ALU.subtract)
                    tt(rz[1], rz[0], rz[0], ALU.mult)
                pending1.append(evac1)

                def evac2(h=h, qt=qt):
                    mm(psb[6][:, :], ones_f, rz[1])
                    rsqrt_ps(ev_rs, psb[6][:, :], 1.0 / 128)
                    stt(ybT[:, h, tsl(qt)], rz[0], pc(PC_SUB), ev_rs, ALU.mult, ALU.mult)
                pending.append(evac2)
        if pending1:
            pending1.pop()()
        if pending:
            pending.pop()()
        if dbg and dbg[0] == f"yb{l}":
            tmpd = VB(0, 2048)
            act(tmpd, ybT[:, 1, :], AF.Copy)
            dump(tmpd, 2048)

        if STOP < 6:
            continue
        dma("sp", zhraw.rearrange("p a b -> p (a b)"), rcv_ap[0][0:128, ZOFF:ZOFF + 8], reads=[("dram", "rcv", l, 0)],
            writes=[zhraw])
        ts(zh, zhraw, hflag, ALU.mult)
        cw = lambda k: pcol[:, l * PC_L + PC_CW + k * 4: l * PC_L + PC_CW + k * 4 + 4]
        tt(tA, zh[:, :, 0], cw(0), ALU.mult)
        tt(tB, zh[:, :, 1], cw(1), ALU.mult)
        tt(Fh[:, :, 0], tA, tB, ALU.add)
        tt(Fh[:, :, 1], zh[:, :, 1], cw(0), ALU.mult)
        tt(tC, cbh, Fh, ALU.mult)
        tt(yc[:, :, 0:2], ycp, tC, ALU.add)
        if dbg and dbg[0] == f"yc{l}":
            tmpd = VB(0, 2048)
            act(tmpd, yc[:, 1, :], AF.Copy)
            dump(tmpd, 2048)

        if STOP < 7:
            continue
        TH = T // 2
        hTh = r3(VB(0, 8 * TH, BF), 8)
        yah = r3(VB(16 * KB, 4 * TH, BF), 4)
        g_t1 = [VB(24 * KB + i * 2 * KB, 512) for i in range(2)]
        g_gv = VB(28 * KB, 512)
        g_ss = zcol[:, 2:4]
        vtok = [VB(30 * KB + i * KB, 512, BF) for i in range(2)]
        mgh = r3(VB(72 * KB, 8 * TH, BF), 8)
        m_g = [VB(88 * KB + i * 2 * KB, 512) for i in range(2)]
        for hh in range(2):
            t0 = hh * TH
            norm_apply(hTh, PC_N1, t0, TH)
            hTh_of = lambda kc, tl: hTh[:, kc, tsl(tl)]
            P7L = int(os.environ.get('K_P7', '9'))
            if P7L < 2:
                continue
            sl_u = slab(w_in_l[:, 0:512], 8, 512)
            sl_av = slab(w_in_l[:, 512:1024], 8, 512)
            gctr = [0]
            for g in range(4):
                def ev_u(tl, ps, g=g):
                    t1 = g_t1[gctr[0] % 2]
                    gctr[0] += 1
                    gelu_from_psum(ps[:, :], yah[:, g, tsl(tl)], t1)
                dense_ws(sl_u, g * 128, 8, hTh_of, range(2), ev_u)
            if P7L < 3:
                continue
            def vproj(tbl):
                ps_ = bank()
                for kc in range(8):
                    mm(ps_[:, :], hTh[:, kc, tbl * 128:(tbl + 1) * 128], sl_av[:, kc, :], start=(kc == 0), stop=(kc == 7))
                return ps_

            ps_next = vproj(0)
            for tbl in range(8):
                ps = ps_next
                t1 = g_t1[gctr[0] % 2]
                gctr[0] += 1
                gelu_from_psum(ps[:, :], g_gv, t1)
                act(t1, g_gv, AF.Square, accum_out=g_ss[:, 0:1])
                rsqrt_ps(g_ss[:, 1:2], g_ss[:, 0:1], 1.0 / 512)
                vt = vtok[tbl % 2]
                stt(vt, g_gv, g_ss[:, 1:2], pb(PB_VN, 512), ALU.mult, ALU.mult)
                if tbl + 1 < 8:
                    ps_next = vproj(tbl + 1)
                if tbl % 4 == 0:
                    psv = [bank() for _ in range(4)]
                for g in range(4):
                    mm(psv[g][:, (tbl % 4) * 128:(tbl % 4 + 1) * 128], vt[:, g * 128:(g + 1) * 128], wsTm[:, l, g * 128:(g + 1) * 128])
                if tbl % 4 == 3:
                    tl = tbl // 4
                    for g in range(4):
                        t1 = g_t1[gctr[0] % 2]
                        gctr[0] += 1
                        for j4 in range(4):
                            tt(t1[:, j4 * 128:(j4 + 1) * 128], pb(PB_BS + g * 128, 128), psv[g][:, j4 * 128:(j4 + 1) * 128], ALU.add)
                        tt(yah[:, g, tsl(tl)], t1, yah[:, g, tsl(tl)], ALU.mult)
            if dbg and dbg[0] == f"ya{l}" and hh == 0:
                tmpd = VB(56 * KB, 1024)
                act(tmpd, yah[:, 1, :], AF.Copy)
                dump(tmpd, 1024)
            if P7L < 4:
                continue
            ysrc = [lambda kc, tl: yah[:, kc, tsl(tl)],
                    lambda kc, tl: ybT[:, kc, t0 + tl * 512: t0 + (tl + 1) * 512],
                    lambda kc, tl: yc[:, kc, t0 + tl * 512: t0 + (tl + 1) * 512]]
            mctr = [0]
            for b in range(3):
                sl_br = slab(w_br_d[l, b], 4, 1024)
                for cg in range(2):
                    sl_g = slab(w_in_l[:, 4096 + b * 1024 + cg * 512: 4096 + b * 1024 + (cg + 1) * 512], 8, 512)
                    for c4 in range(4):
                        c = cg * 4 + c4

                        def ev_g(tl, ps, b=b, c=c):
                            gt = m_g[mctr[0] % 2]
                            mctr[0] += 1
                            act(gt, ps[:, :], AF.Sigmoid, bias=pc(PC_GB + b * 8 + c))
                            psP = bank()
                            for kc in range(4):
                                mm(psP[:, :], sl_br[:, kc, c * 128:(c + 1) * 128], ysrc[b](kc, tl), start=(kc == 0), stop=(kc == 3))
                            if b == 0:
                                tt(mgh[:, c, tsl(tl)], gt, psP[:, :], ALU.mult)
                            else:
                                tt(gt, gt, psP[:, :], ALU.mult)
                                tt(mgh[:, c, tsl(tl)], gt, mgh[:, c, tsl(tl)], ALU.add)
                        dense_ws(sl_g, c4 * 128, 8, hTh_of, range(2), ev_g)
            if P7L < 5:
                continue
            for cg in range(2):
                sl_o = slab(w_out_d[l][:, cg * 512:(cg + 1) * 512], 8, 512)
                for c4 in range(4):
                    co = cg * 4 + c4

                    def ev_o(tl, ps, co=co):
                        xs = XT[:, co, t0 + tl * 512: t0 + (tl + 1) * 512]
                        tt(xs, xs, ps[:, :], ALU.add)
                    dense_ws(sl_o, c4 * 128, 8, lambda kc, tl: mgh[:, kc, tsl(tl)], range(2), ev_o)
        if dbg and dbg[0] == f"xm{l}":
            dump(XT[:, 2, :], 2048)

        if STOP < 8:
            continue
        norm_stats(88 * KB)
        h2T = r3(VB(0, 8 * T, BF), 8)
        norm_apply(h2T, PC_N2, 0, T)
        rT = r3(VB(40 * KB, 8 * T, BF), 8)
        f_t = [VB(72 * KB + i * 2 * KB, 512) for i in range(2)]
        fctr = [0]
        for g in range(4):
            for cg in range(2):
                sl_1 = slab(w_ff1_d[l][:, g * 1024 + cg * 512: g * 1024 + (cg + 1) * 512], 8, 512)
                for c4 in range(4):
                    hc = cg * 4 + c4

                    def ev_f1(t, ps, hc=hc):
                        ft = f_t[fctr[0] % 2]
                        fctr[0] += 1
                        act(ft, ps[:, :], AF.Relu)
                        tt(rT[:, hc, tsl(t)], ft, ft, ALU.mult)
                    dense_ws(sl_1, c4 * 128, 8, lambda kc, t: h2T[:, kc, tsl(t)], range(NTT), ev_f1)
            for cg in range(2):
                sl_2 = slab(w_ff2_d[l][g * 1024:(g + 1) * 1024, cg * 512:(cg + 1) * 512], 8, 512)
                for c4 in range(4):
                    co = cg * 4 + c4

                    def ev_f2(t, ps, co=co):
                        tt(XT[:, co, tsl(t)], XT[:, co, tsl(t)], ps[:, :], ALU.add)
                    dense_ws(sl_2, c4 * 128, 8, lambda kc, t: rT[:, kc, tsl(t)], range(NTT), ev_f2)

    out_toks = []
    for tb in range(16):
        xo = VB(56 * KB + (tb % 4) * 4 * KB, 1024)
        for c4 in range(2):
            ps = bank()
            for j in range(4):
                c = c4 * 4 + j
                tr(ps[:, j * 128:(j + 1) * 128], XT[:, c, tb * 128:(tb + 1) * 128], inc=(j == 3))
            act(xo[:, c4 * 512:(c4 + 1) * 512], ps[:, :], AF.Copy)
        out_toks.append(dma("sp", out_d[tb * 128:(tb + 1) * 128, :], xo, reads=[xo], writes=[("dram", "out", tb)]))
    final_waits = {}
    for s, v in out_toks:
        final_waits[s] = max(final_waits.get(s, 0), v)
    if dbg_d is not None:
        t = S.lastw.get(("dram", "dbg"))
        if t:
            final_waits[t[0]] = max(final_waits.get(t[0], 0), t[1])

    with es:
        with nc.Block() as block:
            def run(name, e):
                for waits, fn, incspec in S.ops[name]:
                    for s, v in waits:
                        e.wait_ge(sems[s], v)
                    ins = fn(e)
                    if incspec is not None:
                        ins.then_inc(sems[incspec[0]], incspec[1])
                if name == "sp":
                    for s, v in final_waits.items():
                        e.wait_ge(sems[s], v)

            @block.tensor
            def _(e):
                run("pe", e)

            @block.scalar
            def _(e):
                run("act", e)

            @block.vector
            def _(e):
                run("dve", e)

            @block.gpsimd
            def _(e):
                run("pool", e)

            @block.sync
            def _(e):
                run("sp", e)
    return nc


def _host_consts(hf):
    cstf = np.zeros((128, CF_N), np.float32)
    cstf[:, CF_ID:CF_ID + 128] = np.eye(128, dtype=np.float32)
    s_idx = np.arange(128)[:, None]
    t_idx = np.arange(128)[None, :]
    cstf[:, CF_TRI:CF_TRI + 128] = (t_idx >= s_idx).astype(np.float32)
    kl = np.arange(128, dtype=np.float64)
    for h in range(4):
        for d in range(16):
            dlt = d * 128 - 1536
            cstf[:, CF_BOWN + h * 16 + d] = SLOPES[h] * (kl + dlt - 256)
        for d in range(28):
            dlt = d * 128 - 3584
            cstf[:, CF_BPRE + h * 28 + d] = SLOPES[h] * (kl + dlt - 256) if hf == 1 else NEG
    cstf[:, CF_HF] = float(hf)
    cstb = np.zeros((128, CB_N), np.float32)
    cstb[:, CB_ONES:CB_ONES + 128] = 1.0
    p = np.arange(128)
    cstb[:, CB_BLK:CB_BLK + 128] = (p[:, None] // 64 == p[None, :] // 64).astype(np.float32)
    cstb[:, CB_ID:CB_ID + 128] = np.eye(128)
    k_ = np.arange(128)[:, None]
    q_ = np.arange(128)[None, :]
    for h in range(4):
        c = np.zeros((128, 128))
        same = (k_ // 64) == (q_ // 64)
        c = np.where((k_ > q_) & same, -16.0 * SLOPES[h] * (k_ - q_), 0.0)
        c = np.where((k_ // 64) > (q_ // 64), 8.0 * NEG, c)
        cstb[:, CB_CORR + h * 128: CB_CORR + (h + 1) * 128] = c
    return cstf, cstb.astype(ml_dtypes.bfloat16)


def _host_params(norm1_w, norm2_w, gate_b, b_qnorm_w, b_knorm_w, c_conv_w, a_vnorm_w, b_subnorm_w, b_lam, a_bs):
    pcol = np.zeros((128, 2 * PC_L), np.float32)
    pbc = np.zeros((128, 2 * PB_L), np.float32)
    for l in range(2):
        o = l * PC_L
        pcol[:, o + PC_N1:o + PC_N1 + 8] = norm1_w[l].reshape(8, 128).T
        pcol[:, o + PC_N2:o + PC_N2 + 8] = norm2_w[l].reshape(8, 128).T
        pcol[:, o + PC_GB:o + PC_GB + 24] = gate_b[l].reshape(24, 128).T
        pcol[:, o + PC_QN] = b_qnorm_w[l].reshape(128)
        pcol[:, o + PC_KN] = b_knorm_w[l].reshape(128)
        pcol[:, o + PC_CW:o + PC_CW + 12] = c_conv_w[l].reshape(12, 128).T
        pcol[:, o + PC_SUB] = b_subnorm_w[l]
        o = l * PB_L
        pbc[:, o + PB_VN:o + PB_VN + 512] = a_vnorm_w[l][None, :]
        pbc[:, o + PB_SUB:o + PB_SUB + 128] = b_subnorm_w[l][None, :]
        pbc[:, o + PB_LAM:o + PB_LAM + 256] = b_lam[l].reshape(256)[None, :]
        pbc[:, o + PB_BS:o + PB_BS + 512] = a_bs[l].reshape(512)[None, :]
    return pcol, pbc


_NC_CACHE = {}


def kernel(x, norm1_w, w_in, gate_b, a_vnorm_w, a_ws, a_bs, b_qnorm_w, b_knorm_w, b_lam,
           b_subnorm_w, c_conv_w, w_br_a, w_br_b, w_br_c, w_out, norm2_w, w_ff1, w_ff2, _dbg=None, _layers=(0, 1)):
    f = lambda a: np.ascontiguousarray(np.asarray(a, dtype=np.float32))
    x = f(x)
    key = (tuple(_layers), _dbg)
    if key not in _NC_CACHE:
        _NC_CACHE[key] = build(list(_layers), True, True, dbg=_dbg)
    nc = _NC_CACHE[key]
    pcol, pbc = _host_params(f(norm1_w), f(norm2_w), f(gate_b), f(b_qnorm_w), f(b_knorm_w), f(c_conv_w),
                             f(a_vnorm_w), f(b_subnorm_w), f(b_lam), f(a_bs))
    w_br = np.ascontiguousarray(np.stack([f(w_br_a), f(w_br_b), f(w_br_c)], axis=1))
    wsT = np.ascontiguousarray(np.transpose(f(a_ws), (0, 3, 1, 2)))
    shared = {"w_in": f(w_in), "w_br": w_br, "w_out": f(w_out), "w_ff1": f(w_ff1), "w_ff2": f(w_ff2),
              "wsT": wsT, "pcol": pcol, "pbc": pbc}
    if os.environ.get("K_LIGHT"):
        for k_ in ("w_in", "w_br", "w_out", "w_ff1", "w_ff2"):
            shared[k_] = np.zeros((2, 8, 8), np.float32)
    in_maps = []
    for c in range(8):
        b, hf = c // 2, c % 2
        cstf, cstb = _host_consts(hf)
        m = dict(shared)
        m["x"] = np.ascontiguousarray(x[b, hf * T:(hf + 1) * T, :])
        m["cstf"] = cstf
        m["cstb"] = cstb
        in_maps.append(m)
    res = run_bass_kernel_spmd(nc, in_maps, core_ids=list(range(8)))
    out = np.empty((4, 4096, D), np.float32)
    for c in range(8):
        b, hf = c // 2, c % 2
        out[b, hf * T:(hf + 1) * T, :] = np.asarray(res.results[c]["out"], dtype=np.float32)
    if _dbg:
        return out, [np.asarray(res.results[c]["dbg"], dtype=np.float32) for c in range(8)]
    return out
```

```python
import math
import os
from contextlib import ExitStack

import numpy as np
import ml_dtypes
import concourse.bass as bass
import concourse.mybir as mybir
from concourse.bass_utils import run_bass_kernel_spmd

F32 = mybir.dt.float32
BF = mybir.dt.bfloat16
AF = mybir.ActivationFunctionType
ALU = mybir.AluOpType

T = 2048
D = 1024
NTT = 4
EPS = 1e-6
NEG = -30000.0
SLOPES = [2.0 ** (-8.0 * (h + 1) / 4) for h in range(4)]
FKV = 2048 + 16 * 129 + 16
VOFF = 2048
ZOFF = 2048 + 2064
NDMA = 48
KB = 1024

PC_N1, PC_N2, PC_GB, PC_QN, PC_KN, PC_CW, PC_SUB, PC_L = 0, 8, 16, 40, 41, 42, 54, 55
PB_VN, PB_SUB, PB_LAM, PB_BS, PB_L = 0, 512, 640, 896, 1408
CF_ID, CF_TRI, CF_BOWN, CF_BPRE, CF_HF, CF_N = 0, 128, 256, 320, 432, 436
CB_ONES, CB_BLK, CB_ID, CB_CORR, CB_N = 0, 128, 256, 384, 896


class Sched:
    def __init__(self):
        self.ops = {e: [] for e in ("pe", "act", "dve", "pool", "sp")}
        self.cnt = {e: 0 for e in self.ops}
        self.waited = {e: {} for e in self.ops}
        self.lastw = {}
        self.readers = {}
        self.dma_next_q = {"sp": 0, "pool": 0}
        self.dma_cnt = [0] * NDMA

    @staticmethod
    def keys(ap):
        if isinstance(ap, tuple):
            return [ap]
        dsz = 4 if ap.dtype == F32 else 2
        dims = ap.ap
        pstride = dims[0][0]
        off = ap.offset % pstride if pstride else ap.offset
        hi = off + sum((c - 1) * st for st, c in dims[1:]) + 1
        lo_b, hi_b = off * dsz, hi * dsz
        if str(ap.space) == "PSUM":
            return [("p", ap.tensor.name)]
        base = ("s",)
        return [base + (b,) for b in range(lo_b // 256, (hi_b - 1) // 256 + 1)]

    def op(self, eng, fn, reads=(), writes=(), inc=True, dma=False, sem=None):
        inc = True
        rk = [k for a in reads if a is not None for k in self.keys(a)]
        wk = [k for a in writes if a is not None for k in self.keys(a)]
        deps = {}

        def add(s, v):
            if deps.get(s, 0) < v:
                deps[s] = v

        for k in rk:
            t = self.lastw.get(k)
            if t:
                add(*t)
            if k[0] == "p":
                for s, v in self.readers.get(k, {}).items():
                    if s != eng:
                        add(s, v)
        for k in wk:
            t = self.lastw.get(k)
            if t:
                add(*t)
            for s, v in self.readers.get(k, {}).items():
                add(s, v)
        if eng == "pe":
            deps.pop("pe", None)
        waits = []
        wd = self.waited[eng]
        for s, v in deps.items():
            if wd.get(s, 0) < v:
                waits.append((s, v))
                wd[s] = v
        if dma:
            half = NDMA // 2
            base = 0 if eng == "sp" else half
            j = base + self.dma_next_q[eng]
            self.dma_next_q[eng] = (self.dma_next_q[eng] + 1) % half
            sn = ("dma", j)
            prev = self.dma_cnt[j]
            if prev and wd.get(sn, 0) < prev:
                waits.append((sn, prev))
                wd[sn] = prev
            self.dma_cnt[j] += 16
            tok = (sn, self.dma_cnt[j])
            incspec = (sn, 16)
        elif sem is not None:
            tok = (sem, 1)
            incspec = (sem, 1)
        elif inc:
            self.cnt[eng] += 1
            tok = (eng, self.cnt[eng])
            incspec = (eng, 1)
        else:
            tok = (eng, self.cnt[eng] + 1)
            incspec = None
        self.ops[eng].append((waits, fn, incspec))
        for k in wk:
            self.lastw[k] = tok
            self.readers[k] = {}
        for k in rk:
            r = self.readers.setdefault(k, {})
            if r.get(tok[0], 0) < tok[1]:
                r[tok[0]] = tok[1]
        return tok


def build(layers, first, last, dbg=None):
    nc = bass.Bass("TRN2", target_bir_lowering=False)
    S = Sched()
    x_d = nc.dram_tensor("x", [T, D], F32, kind="ExternalInput").ap()
    out_d = nc.dram_tensor("out", [T, D], F32, kind="ExternalOutput").ap()
    LIGHT = bool(os.environ.get("K_LIGHT"))
    wshape = (lambda s_: [2, 8, 8]) if LIGHT else (lambda s_: s_)
    w_in_d = nc.dram_tensor("w_in", wshape([2, D, 7168]), F32, kind="ExternalInput").ap()
    w_br_d = nc.dram_tensor("w_br", wshape([2, 3, 512, D]), F32, kind="ExternalInput").ap()
    w_out_d = nc.dram_tensor("w_out", wshape([2, D, D]), F32, kind="ExternalInput").ap()
    w_ff1_d = nc.dram_tensor("w_ff1", wshape([2, D, 4096]), F32, kind="ExternalInput").ap()
    w_ff2_d = nc.dram_tensor("w_ff2", wshape([2, 4096, D]), F32, kind="ExternalInput").ap()
    if LIGHT:
        class _Dummy:
            def __getitem__(self, k):
                return self
        w_in_d = w_br_d = w_out_d = w_ff1_d = w_ff2_d = _Dummy()
    wsT_d = nc.dram_tensor("wsT", [2, 128, 4, 128], F32, kind="ExternalInput").ap()
    pcol_d = nc.dram_tensor("pcol", [128, 2 * PC_L], F32, kind="ExternalInput").ap()
    pbc_d = nc.dram_tensor("pbc", [128, 2 * PB_L], F32, kind="ExternalInput").ap()
    cstf_d = nc.dram_tensor("cstf", [128, CF_N], F32, kind="ExternalInput").ap()
    cstb_d = nc.dram_tensor("cstb", [128, CB_N], BF, kind="ExternalInput").ap()
    snd = {(l, h): nc.dram_tensor(f"snd{l}_{h}", [128, FKV], BF) for l in layers for h in range(4)}
    rcv = {(l, h): nc.dram_tensor(f"rcv{l}_{h}", [256, FKV], BF) for l in layers for h in range(4)}
    dbg_d = None
    if dbg:
        dbg_d = nc.dram_tensor("dbg", [128, dbg[1]], F32, kind="ExternalOutput").ap()

    es = ExitStack()
    ARENA_B = 206 * KB
    arena = es.enter_context(nc.sbuf_tensor("arena", [128, ARENA_B // 4], F32))
    psb = [es.enter_context(nc.psum_tensor(f"ps{i}", [128, 512], F32)) for i in range(8)]
    sems = {e: es.enter_context(nc.semaphore(f"s_{e}")) for e in ("pe", "act", "dve", "pool", "sp")}
    for j in range(NDMA):
        sems[("dma", j)] = es.enter_context(nc.semaphore(f"s_dma{j}"))
    for l in layers:
        for h in range(4):
            sems[("cc", l, h)] = es.enter_context(nc.semaphore(f"s_cc{l}_{h}"))

    def V(off, n, dt=F32):
        assert off % 4 == 0 and off + n * (4 if dt == F32 else 2) <= ARENA_B, (off, n)
        if dt == F32:
            return arena[:, off // 4: off // 4 + n]
        assert n % 2 == 0
        return arena[:, off // 4: off // 4 + n // 2].bitcast(BF)

    def r3(ap, a):
        return ap.rearrange("p (a b) -> p a b", a=a)

    XT = r3(V(0, 8 * T), 8)
    RING0 = 64 * KB
    C0 = 96 * KB
    cstf = V(C0, CF_N)
    cstb = V(C0 + 1792, CB_N, BF)
    pcol = V(C0 + 3584, 2 * PC_L)
    pbc = V(C0 + 4096, 2 * PB_L)
    o = C0 + 4096 + 2 * PB_L * 4
    wsTm = r3(V(o, 2 * 512, BF), 2); o += 2048
    lamcol = V(o, 8); o += 32
    zcol = V(o, 8); o += 32
    ones_f = V(o, 128); o += 512
    assert o <= 114 * KB
    B0 = 114 * KB
    BSZ = ARENA_B - B0

    def VB(off, n, dt=F32):
        assert off + n * (4 if dt == F32 else 2) <= BSZ, (off, n, dt)
        return V(B0 + off, n, dt)

    ident_f = cstf[:, CF_ID:CF_ID + 128]
    tri_f = cstf[:, CF_TRI:CF_TRI + 128]
    hflag = cstf[:, CF_HF:CF_HF + 1]
    ones_b = cstb[:, CB_ONES:CB_ONES + 128]
    blk_b = cstb[:, CB_BLK:CB_BLK + 128]
    ident_b = cstb[:, CB_ID:CB_ID + 128]
    corr_b = r3(cstb[:, CB_CORR:CB_CORR + 512], 4)

    def mm(out, lhsT, rhs, start=True, stop=True, inc=None, skip=False):
        if inc is None:
            inc = stop
        S.op("pe", lambda e: e.matmul(out, lhsT, rhs, start=start, stop=stop, skip_group_check=skip),
             reads=[lhsT, rhs], writes=[out], inc=inc)

    def tr(out, in_, inc=True):
        S.op("pe", lambda e: e.transpose(out, in_, ident_f), reads=[in_, ident_f], writes=[out], inc=inc)

    def act(out, in_, func, bias=None, scale=None, accum_out=None):
        kw = {}
        rd = [in_]
        if bias is not None:
            kw["bias"] = bias
            if not isinstance(bias, float):
                rd.append(bias)
        if scale is not None:
            kw["scale"] = scale
        wr = [out]
        if accum_out is not None:
            kw["accum_out"] = accum_out
            wr.append(accum_out)
        S.op("act", lambda e: e.activation(out, in_, func, **kw), reads=rd, writes=wr)

    def tt(out, in0, in1, op, eng="dve"):
        S.op(eng, lambda e: e.tensor_tensor(out, in0, in1, op), reads=[in0, in1], writes=[out])

    def ts(out, in0, s1, op0, s2=None, op1=None, eng="dve"):
        rd = [in0] + [s for s in (s1, s2) if s is not None and not isinstance(s, float)]
        if op1 is None:
            S.op(eng, lambda e: e.tensor_scalar(out, in0, s1, None, op0), reads=rd, writes=[out])
        else:
            S.op(eng, lambda e: e.tensor_scalar(out, in0, s1, s2, op0, op1), reads=rd, writes=[out])

    def stt(out, in0, scalar, in1, op0, op1):
        rd = [in0, in1] + ([] if isinstance(scalar, float) else [scalar])
        S.op("dve", lambda e: e.scalar_tensor_tensor(out, in0, scalar, in1, op0, op1), reads=rd, writes=[out])

    def recip(out, in_):
        S.op("dve", lambda e: e.reciprocal(out, in_), reads=[in_], writes=[out])

    def memset(ap, val):
        S.op("dve", lambda e: e.memset(ap, val), writes=[ap])

    def dma(q, out, in_, reads=(), writes=()):
        return S.op(q, lambda e: e.dma_start(out=out, in_=in_), reads=list(reads), writes=list(writes), dma=True)

    bank_ctr = [0]

    def bank():
        b = psb[bank_ctr[0] % 8]
        bank_ctr[0] += 1
        return b

    ring_ctr = [0]

    def slab(src, kc, ncol):
        assert kc * ncol * 2 <= 8 * KB
        slot = ring_ctr[0] % 4
        ring_ctr[0] += 1
        dst = r3(V(RING0 + slot * 8 * KB, kc * ncol, BF), kc)
        srcv = None if LIGHT else src.rearrange("(k p) n -> p k n", p=128)
        for k_ in range(kc):
            if LIGHT:
                continue
            dma("pool", dst[:, k_, :], srcv[:, k_, :], writes=[dst[:, k_, :]])
        return dst

    def tsl(t):
        return slice(t * 512, (t + 1) * 512)

    def dump(ap, ncols):
        if dbg_d is not None:
            dma("sp", dbg_d[:, 0:ncols], ap, reads=[ap], writes=[("dram", "dbg")])

    def rsqrt_ps(out, ps_in, scale):
        act(out, ps_in, AF.Sqrt, bias=epscol, scale=scale)
        recip(out, out)

    epscol = zcol[:, 0:1]

    dma("sp", cstf, cstf_d, writes=[cstf])
    dma("sp", cstb, cstb_d, writes=[cstb])
    dma("sp", pcol, pcol_d, writes=[pcol])
    dma("sp", pbc, pbc_d, writes=[pbc])
    memset(zcol[:, 0:1], EPS)
    memset(ones_f, 1.0)

    def gelu_from_psum(ps, out, t1):
        act(out, ps, AF.Gelu_apprx_tanh)

    if first:
        for tb in range(16):
            xin = VB(56 * KB + (tb % 4) * 4 * KB, 1024)
            dma("sp", xin, x_d[tb * 128:(tb + 1) * 128, :], writes=[xin])
            for c4 in range(2):
                ps = bank()
                for j in range(4):
                    c = c4 * 4 + j
                    tr(ps[:, j * 128:(j + 1) * 128], xin[:, c * 128:(c + 1) * 128], inc=(j == 3))
                act(XT[:, c4 * 4:(c4 + 1) * 4, tb * 128:(tb + 1) * 128], r3(ps[:, :], 4), AF.Copy)
    else:
        raise NotImplementedError

    STOP = int(os.environ.get('K_STOP', '99'))
    for l in layers:
        lam_init = 0.8 - 0.6 * math.exp(-0.3 * l)
        pc = lambda o_, n=1: pcol[:, l * PC_L + o_: l * PC_L + o_ + n]
        pb = lambda o_, n: pbc[:, l * PB_L + o_: l * PB_L + o_ + n]
        w_in_l = w_in_d[l]

        lam = lamcol[:, 4 * l:4 * l + 1]
        lt = VB(56 * KB, 128)
        if STOP < -3:
            continue
        tt(lt[:, 0:64], pb(PB_LAM, 64), pb(PB_LAM + 64, 64), ALU.mult)
        tt(lt[:, 64:128], pb(PB_LAM + 128, 64), pb(PB_LAM + 192, 64), ALU.mult)
        S.op("dve", lambda e, a=lamcol[:, 4 * l + 1:4 * l + 3], b=r3(lt, 2): e.tensor_reduce(
            a, b, mybir.AxisListType.X, ALU.add), reads=[lt], writes=[lamcol[:, 4 * l + 1:4 * l + 3]])
        act(lamcol[:, 4 * l + 1:4 * l + 3], lamcol[:, 4 * l + 1:4 * l + 3], AF.Exp)
        tt(lam, lamcol[:, 4 * l + 1:4 * l + 2], lamcol[:, 4 * l + 2:4 * l + 3], ALU.subtract)
        ts(lam, lam, lam_init, ALU.add)
        ts(pb(PB_SUB, 128), pb(PB_SUB, 128), 1.0 - lam_init, ALU.mult)
        ts(pc(PC_SUB), pc(PC_SUB), 1.0 - lam_init, ALU.mult)
        if STOP < -2:
            continue
        wst = VB(56 * KB + 1024, 512)
        dma("sp", wst, wsT_d[l].rearrange("s g t -> s (g t)"), writes=[wst])
        for g in range(4):
            tt(wsTm[:, l, g * 128:(g + 1) * 128], wst[:, g * 128:(g + 1) * 128], tri_f, ALU.mult)

        hT = r3(VB(0, 8 * T, BF), 8)
        rstd = VB(32 * KB, T)
        yc = r3(VB(40 * KB, 4 * T, BF), 4)

        def norm_stats(sqoff):
            for t in range(NTT):
                ps = bank()
                for c in range(8):
                    sq = VB(sqoff + (c % 2) * KB, 512, BF)
                    act(sq, XT[:, c, tsl(t)], AF.Square)
                    mm(ps[:, :], ones_b, sq, start=(c == 0), stop=(c == 7))
                rsqrt_ps(rstd[:, tsl(t)], ps[:, :], 1.0 / D)

        def norm_apply(dst, wo, t0, n):
            for tq in range(n // 512):
                a, b = tq * 512, (tq + 1) * 512
                for c in range(8):
                    stt(dst[:, c, a:b], XT[:, c, t0 + a:t0 + b], pc(wo + c), rstd[:, t0 + a:t0 + b], ALU.mult, ALU.mult)

        if STOP < -1:
            continue
        norm_stats(88 * KB)
        norm_apply(hT, PC_N1, 0, T)
        if dbg and dbg[0] == f"hT{l}":
            tmpd = VB(56 * KB, 2048)
            act(tmpd, hT[:, 3, :], AF.Copy)
            dump(tmpd, 2048)

        def dense_ws(sl, col0, kcn, rhs_of, tts, evac):
            for t in tts:
                ps = bank()
                for kc in range(kcn):
                    mm(ps[:, :], sl[:, kc, col0:col0 + 128], rhs_of(kc, t), start=(kc == 0), stop=(kc == kcn - 1))
                evac(t, ps)

        hT_of = lambda kc, t: hT[:, kc, tsl(t)]

        if STOP < 1:
            continue
        sl_cc = slab(w_in_l[:, 3072:3584], 8, 512)
        sl_cx = slab(w_in_l[:, 3584:4096], 8, 512)
        sl_cb = slab(w_in_l[:, 2560:3072], 8, 512)
        zbuf = VB(56 * KB, 2052)
        accb = VB(56 * KB + 8208, 2048)
        smallb = VB(73984, 64)
        cbtmp = VB(78 * KB, 64)
        cbh = r3(smallb[:, 0:8], 4)
        ycp = r3(smallb[:, 8:16], 4)
        Fh = r3(smallb[:, 16:24], 4)
        zh = r3(smallb[:, 24:32], 4)
        tA = smallb[:, 32:36]
        tB = smallb[:, 36:40]
        tC = r3(smallb[:, 40:48], 4)
        zsend16 = VB(73984 + 256, 16, BF)
        zsend = r3(zsend16[:, 0:8], 4)
        zhraw = r3(VB(73984 + 256 + 32, 8, BF), 4)
        memset(zsend16, 0.0)
        cct = [VB(74 * KB + i * 2 * KB, 512) for i in range(2)]
        memset(zbuf[:, 0:2], 0.0)
        cc_ctr = [0]
        for i in range(4):
            def ev_cc(t, ps, i=i):
                c = cct[cc_ctr[0] % 2]
                cc_ctr[0] += 1
                act(c, ps[:, :], AF.Copy)
                psB = bank()
                for kc in range(8):
                    mm(psB[:, :], sl_cx[:, kc, i * 128:(i + 1) * 128], hT_of(kc, t), start=(kc == 0), stop=(kc == 7))
                tt(zbuf[:, 2 + t * 512: 2 + (t + 1) * 512], c, psB[:, :], ALU.mult)
            dense_ws(sl_cc, i * 128, 8, hT_of, range(NTT), ev_cc)
            P1L = int(os.environ.get("K_P1", "9"))
            if P1L < 2:
                continue
            w0, w1, w2 = pc(PC_CW + 0 * 4 + i), pc(PC_CW + 1 * 4 + i), pc(PC_CW + 2 * 4 + i)
            ts(accb, zbuf[:, 2:2050], w2, ALU.mult)
            if P1L < 3:
                continue
            stt(accb, zbuf[:, 1:2049], w1, accb, ALU.mult, ALU.add)
            stt(accb, zbuf[:, 0:2048], w0, accb, ALU.mult, ALU.add)
            if P1L < 4:
                continue
            act(zsend[:, i, :], zbuf[:, 2048:2050], AF.Copy)
            if P1L < 5:
                continue

            def ev_cb(t, ps, i=i):
                tt(yc[:, i, tsl(t)], accb[:, tsl(t)], ps[:, :], ALU.mult)
                TINY = int(os.environ.get('K_TINY', '9'))
                if t == 0 and TINY >= 1:
                    if os.environ.get('K_CBACT'):
                        act(cbtmp, ps[:, 0:64], AF.Copy)
                    else:
                        ts(cbtmp, ps[:, 0:64], 1.0, ALU.mult)
                    if TINY >= 2:
                        act(cbh[:, i, :], cbtmp[:, 0:2], AF.Copy)
                    if TINY >= 3:
                        tt(ycp[:, i, :], cbtmp[:, 0:2], accb[:, 0:2], ALU.mult)
            dense_ws(sl_cb, i * 128, 8, hT_of, range(NTT), ev_cb)

        if STOP < 2:
            continue
        qk_pending = []

        def qk_flush():
            while qk_pending:
                qk_pending.pop(0)()

        def qknorm_evac(dst_of, wcol, tmpoff):
            def ev(t, ps, h):
                qk_flush()
                sq = VB(tmpoff, 512, BF)
                act(sq, ps[:, :], AF.Square)

                def stage2():
                    ps2 = bank()
                    mm(ps2[:, :], blk_b, sq)
                    rt = VB(tmpoff + KB, 512)
                    rsqrt_ps(rt, ps2[:, :], 1.0 / 64)
                    stt(dst_of(h, t), ps[:, :], wcol, rt, ALU.mult, ALU.mult)
                stage2()
            return ev

        sl_k = slab(w_in_l[:, 1536:2048], 8, 512)
        sl_v = slab(w_in_l[:, 2048:2560], 8, 512)
        kst = [VB(56 * KB + i * 4 * KB, 2048, BF) for i in range(2)]
        vst = [VB(64 * KB + i * 4224, 4 * 4 * 129, BF).rearrange("p (h t e) -> p h t e", h=4, t=4) for i in range(2)]
        for i in range(2):
            memset(vst[i][:, :, :, 128:129], 1.0)
        snd_ap = [snd[(l, h)].ap() for h in range(4)]
        rcv_ap = [rcv[(l, h)].ap() for h in range(4)]
        kev = qknorm_evac(lambda h, t: kst[h % 2][:, tsl(t)], pc(PC_KN), 73 * KB)
        for h in range(4):
            dense_ws(sl_k, h * 128, 8, hT_of, range(NTT), lambda t, ps, h=h: kev(t, ps, h))
            qk_flush()
            dma("sp", snd_ap[h][:, 0:2048], kst[h % 2], reads=[kst[h % 2]],
                writes=[("dram", "snd", l, "k", h)])
        sl_q = slab(w_in_l[:, 1024:1536], 8, 512)
        vdst = [snd_ap[h][:, VOFF:VOFF + 2064].rearrange("p (t e) -> p t e", t=16) for h in range(4)]
        for tb in range(16):
            ps = bank()
            for kc in range(8):
                mm(ps[:, :], hT[:, kc, tb * 128:(tb + 1) * 128], sl_v[:, kc, :], start=(kc == 0), stop=(kc == 7))
            vq = vst[(tb // 4) % 2]
            act(vq[:, :, tb % 4, 0:128], r3(ps[:, :], 4), AF.Copy)
            if tb % 4 == 3:
                q4 = tb // 4
                for h in range(4):
                    dma("sp", vdst[h][:, q4 * 4:(q4 + 1) * 4, :], vq[:, h, :, :], reads=[vq],
                        writes=[("dram", "snd", l, "v", h, q4)])
        dma("sp", snd_ap[0][:, ZOFF:ZOFF + 16], zsend16, reads=[zsend16], writes=[("dram", "snd", l, "z")])
        for h in range(1, 4):
            dma("sp", snd_ap[h][:, ZOFF:ZOFF + 16], zsend16, reads=[zsend16], writes=[("dram", "snd", l, "z", h)])
        for h in range(4):
            allsnd = [("dram", "snd", l, "k", h)] + [("dram", "snd", l, "v", h, q) for q in range(4)]
            allsnd.append(("dram", "snd", l, "z") if h == 0 else ("dram", "snd", l, "z", h))
            S.op("pool", lambda e, a=snd[(l, h)], b=rcv[(l, h)]: e.collective_compute(
                "AllGather", ALU.bypass, replica_groups=[[0, 1], [2, 3], [4, 5], [6, 7]],
                ins=[a.ap().opt()], outs=[b.ap().opt()]), reads=allsnd, writes=[("dram", "rcv", l, h)],
                sem=("cc", l, h))

        if STOP < 3:
            continue
        QT = r3(VB(76 * KB, 4 * T, BF), 4)
        qev = qknorm_evac(lambda h, t: QT[:, h, tsl(t)], pc(PC_QN), 73 * KB)
        for h in range(4):
            dense_ws(sl_q, h * 128, 8, hT_of, range(NTT), lambda t, ps, h=h: qev(t, ps, h))
        qk_flush()

        if STOP < 5:
            continue
        kown = VB(0, 2048, BF)
        kpre = VB(4 * KB, 2048, BF)
        vown = r3(VB(8 * KB, 16 * 129, BF), 16)
        vpre = r3(VB(8 * KB + 4128, 16 * 129, BF), 16)
        Et = [[VB(17 * KB + (i * 2 + m) * KB, 512, BF) for m in range(2)] for i in range(3)]
        ev_rz = VB(23 * KB, 2)
        ev_ss = VB(23 * KB + 16, 2)
        ev_tmp = VB(23 * KB + 256, 128)
        ev_o = VB(23 * KB + 256 + 512, 128)
        ev_y = [VB(24 * KB + 256 + i * 512, 128) for i in range(4)]
        ybT = r3(VB(56 * KB, 4 * T, BF), 4)
        OT = [psb[4 + m][:, :] for m in range(2)]
        ev_rs = VB(73 * KB, 512)
        pending = []
        pending1 = []
        Zacc = [VB(23 * KB + m * 2 * KB, 512) for m in range(2)]
        rz = [VB(27 * KB + m * 2 * KB, 512) for m in range(2)]
        bown = cstf[:, CF_BOWN:CF_BOWN + 64]
        bpre = cstf[:, CF_BPRE:CF_BPRE + 112]
        for h in range(4):
            dma("sp", kown, snd_ap[h][:, 0:2048], reads=[("dram", "snd", l, "k", h)], writes=[kown])
            vsrc = snd_ap[h][:, VOFF:VOFF + 2064].rearrange("p (t e) -> p t e", t=16)
            dma("sp", vown, vsrc, reads=[("dram", "snd", l, "v", h, q) for q in range(4)], writes=[vown])
            dma("sp", kpre, rcv_ap[h][0:128, 0:2048], reads=[("dram", "rcv", l, h)], writes=[kpre])
            vsrc2 = rcv_ap[h][0:128, VOFF:VOFF + 2064].rearrange("p (t e) -> p t e", t=16)
            dma("sp", vpre, vsrc2, reads=[("dram", "rcv", l, h)], writes=[vpre])
            for qt in range(4):
                steps = [("d", j) for j in range(4)] + [("o", kt) for kt in range(4 * qt)] + [("p", kt) for kt in range(16)]
                ns = len(steps)

                def qk(i):
                    kind, a = steps[i]
                    for m in range(2):
                        Sb = psb[2 * (i % 2) + m]
                        rows = slice(m * 64, (m + 1) * 64)
                        if kind == "d":
                            kt = 4 * qt + a
                            ks = kown[rows, kt * 128:(kt + 1) * 128]
                            q0 = qt * 512 + a * 128
                            mm(Sb[:, a * 128:(a + 1) * 128], ks, QT[rows, h, q0:q0 + 128], start=True, stop=False, inc=False)
                            mm(Sb[:, a * 128:(a + 1) * 128], ident_b, corr_b[:, h, :], start=False, stop=True, inc=(a == 3))
                            if a < 3:
                                mm(Sb[:, (a + 1) * 128:512], ks, QT[rows, h, q0 + 128:(qt + 1) * 512])
                        else:
                            kb = kown if kind == "o" else kpre
                            mm(Sb[:, :], kb[rows, a * 128:(a + 1) * 128], QT[rows, h, tsl(qt)])

                def ex(i):
                    kind, a = steps[i]
                    if kind == "p":
                        dlt = 128 * a - 2048 - 512 * qt
                        bc = bpre[:, h * 28 + (dlt + 3584) // 128: h * 28 + (dlt + 3584) // 128 + 1]
                    else:
                        kt = 4 * qt + a if kind == "d" else a
                        dlt = 128 * kt - 512 * qt
                        bc = bown[:, h * 16 + (dlt + 1536) // 128: h * 16 + (dlt + 1536) // 128 + 1]
                    c0 = a * 128 if kind == "d" else 0
                    for m in range(2):
                        act(Et[i % 3][m][:, c0:512], psb[2 * (i % 2) + m][:, c0:512], AF.Exp, bias=bc, scale=0.125)

                def av(i):
                    kind, a = steps[i]
                    c0 = a * 128 if kind == "d" else 0
                    if kind == "d":
                        vt = vown[:, 4 * qt + a, 0:128]
                    elif kind == "o":
                        vt = vown[:, a, 0:128]
                    else:
                        vt = vpre[:, a, 0:128]
                    for m in range(2):
                        mm(OT[m][:, c0:512], vt, Et[i % 3][m][:, c0:512], start=(i == 0), stop=(i == ns - 1), skip=True)
                        if i == 0:
                            ts(Zacc[m], Et[i % 3][m], 1.0, ALU.mult)
                        else:
                            tt(Zacc[m][:, c0:512], Zacc[m][:, c0:512], Et[i % 3][m][:, c0:512], ALU.add)

                def filler(i):
                    if i >= 1:
                        for s in range(4):
                            mm(psb[4 + s][:, 258:512], ident_b, cstb[:, CB_CORR:CB_CORR + 254],
                               start=False, stop=False, skip=True)

                qk(0)
                for i in range(ns):
                    ex(i)
                    if i + 1 < ns:
                        qk(i + 1)
                    if i == 0 and pending1:
                        pending1.pop()()
                    av(i)
                    if i == 2 and pending:
                        pending.pop()()
                for m in range(2):
                    act(rz[m], OT[m], AF.Copy)

                def evac1():
                    for m in range(2):
                        mm(psb[6 + m][:, :], ones_f, Zacc[m])
                    for m in range(2):
                        recip(psb[6 + m][:, :], psb[6 + m][:, :])
                        tt(rz[m], rz[m], psb[6 + m][:, :], ALU.mult)
                    ts(rz[1], rz[1], lam, ALU.mult)
                    tt(rz[0], rz[0], rz[1], ALU.subtract)
                    tt(rz[1], rz[0], rz[0], ALU.mult)
                pending1.append(evac1)

                def evac2(h=h, qt=qt):
                    mm(psb[6][:, :], ones_f, rz[1])
                    rsqrt_ps(ev_rs, psb[6][:, :], 1.0 / 128)
                    stt(ybT[:, h, tsl(qt)], rz[0], pc(PC_SUB), ev_rs, ALU.mult, ALU.mult)
                pending.append(evac2)
        if pending1:
            pending1.pop()()
        if pending:
            pending.pop()()
        if dbg and dbg[0] == f"yb{l}":
            tmpd = VB(0, 2048)
            act(tmpd, ybT[:, 1, :], AF.Copy)
            dump(tmpd, 2048)

        if STOP < 6:
            continue
        dma("sp", zhraw.rearrange("p a b -> p (a b)"), rcv_ap[0][0:128, ZOFF:ZOFF + 8], reads=[("dram", "rcv", l, 0)],
            writes=[zhraw])
        ts(zh, zhraw, hflag, ALU.mult)
        cw = lambda k: pcol[:, l * PC_L + PC_CW + k * 4: l * PC_L + PC_CW + k * 4 + 4]
        tt(tA, zh[:, :, 0], cw(0), ALU.mult)
        tt(tB, zh[:, :, 1], cw(1), ALU.mult)
        tt(Fh[:, :, 0], tA, tB, ALU.add)
        tt(Fh[:, :, 1], zh[:, :, 1], cw(0), ALU.mult)
        tt(tC, cbh, Fh, ALU.mult)
        tt(yc[:, :, 0:2], ycp, tC, ALU.add)
        if dbg and dbg[0] == f"yc{l}":
            tmpd = VB(0, 2048)
            act(tmpd, yc[:, 1, :], AF.Copy)
            dump(tmpd, 2048)

        if STOP < 7:
            continue
        TH = T // 2
        hTh = r3(VB(0, 8 * TH, BF), 8)
        yah = r3(VB(16 * KB, 4 * TH, BF), 4)
        g_t1 = [VB(24 * KB + i * 2 * KB, 512) for i in range(2)]
        g_gv = VB(28 * KB, 512)
        g_ss = zcol[:, 2:4]
        vtok = [VB(30 * KB + i * KB, 512, BF) for i in range(2)]
        mgh = r3(VB(72 * KB, 8 * TH, BF), 8)
        m_g = [VB(88 * KB + i * 2 * KB, 512) for i in range(2)]
        for hh in range(2):
            t0 = hh * TH
            norm_apply(hTh, PC_N1, t0, TH)
            hTh_of = lambda kc, tl: hTh[:, kc, tsl(tl)]
            P7L = int(os.environ.get('K_P7', '9'))
            if P7L < 2:
                continue
            sl_u = slab(w_in_l[:, 0:512], 8, 512)
            sl_av = slab(w_in_l[:, 512:1024], 8, 512)
            gctr = [0]
            for g in range(4):
                def ev_u(tl, ps, g=g):
                    t1 = g_t1[gctr[0] % 2]
                    gctr[0] += 1
                    gelu_from_psum(ps[:, :], yah[:, g, tsl(tl)], t1)
                dense_ws(sl_u, g * 128, 8, hTh_of, range(2), ev_u)
            if P7L < 3:
                continue
            def vproj(tbl):
                ps_ = bank()
                for kc in range(8):
                    mm(ps_[:, :], hTh[:, kc, tbl * 128:(tbl + 1) * 128], sl_av[:, kc, :], start=(kc == 0), stop=(kc == 7))
                return ps_

            ps_next = vproj(0)
            for tbl in range(8):
                ps = ps_next
                t1 = g_t1[gctr[0] % 2]
                gctr[0] += 1
                gelu_from_psum(ps[:, :], g_gv, t1)
                act(t1, g_gv, AF.Square, accum_out=g_ss[:, 0:1])
                rsqrt_ps(g_ss[:, 1:2], g_ss[:, 0:1], 1.0 / 512)
                vt = vtok[tbl % 2]
                stt(vt, g_gv, g_ss[:, 1:2], pb(PB_VN, 512), ALU.mult, ALU.mult)
                if tbl + 1 < 8:
                    ps_next = vproj(tbl + 1)
                if tbl % 4 == 0:
                    psv = [bank() for _ in range(4)]
                for g in range(4):
                    mm(psv[g][:, (tbl % 4) * 128:(tbl % 4 + 1) * 128], vt[:, g * 128:(g + 1) * 128], wsTm[:, l, g * 128:(g + 1) * 128])
                if tbl % 4 == 3:
                    tl = tbl // 4
                    for g in range(4):
                        t1 = g_t1[gctr[0] % 2]
                        gctr[0] += 1
                        for j4 in range(4):
                            tt(t1[:, j4 * 128:(j4 + 1) * 128], pb(PB_BS + g * 128, 128), psv[g][:, j4 * 128:(j4 + 1) * 128], ALU.add)
                        tt(yah[:, g, tsl(tl)], t1, yah[:, g, tsl(tl)], ALU.mult)
            if dbg and dbg[0] == f"ya{l}" and hh == 0:
                tmpd = VB(56 * KB, 1024)
                act(tmpd, yah[:, 1, :], AF.Copy)
                dump(tmpd, 1024)
            if P7L < 4:
                continue
            ysrc = [lambda kc, tl: yah[:, kc, tsl(tl)],
                    lambda kc, tl: ybT[:, kc, t0 + tl * 512: t0 + (tl + 1) * 512],
                    lambda kc, tl: yc[:, kc, t0 + tl * 512: t0 + (tl + 1) * 512]]
            mctr = [0]
            for b in range(3):
                sl_br = slab(w_br_d[l, b], 4, 1024)
                for cg in range(2):
                    sl_g = slab(w_in_l[:, 4096 + b * 1024 + cg * 512: 4096 + b * 1024 + (cg + 1) * 512], 8, 512)
                    for c4 in range(4):
                        c = cg * 4 + c4

                        def ev_g(tl, ps, b=b, c=c):
                            gt = m_g[mctr[0] % 2]
                            mctr[0] += 1
                            act(gt, ps[:, :], AF.Sigmoid, bias=pc(PC_GB + b * 8 + c))
                            psP = bank()
                            for kc in range(4):
                                mm(psP[:, :], sl_br[:, kc, c * 128:(c + 1) * 128], ysrc[b](kc, tl), start=(kc == 0), stop=(kc == 3))
                            if b == 0:
                                tt(mgh[:, c, tsl(tl)], gt, psP[:, :], ALU.mult)
                            else:
                                tt(gt, gt, psP[:, :], ALU.mult)
                                tt(mgh[:, c, tsl(tl)], gt, mgh[:, c, tsl(tl)], ALU.add)
                        dense_ws(sl_g, c4 * 128, 8, hTh_of, range(2), ev_g)
            if P7L < 5:
                continue
            for cg in range(2):
                sl_o = slab(w_out_d[l][:, cg * 512:(cg + 1) * 512], 8, 512)
                for c4 in range(4):
                    co = cg * 4 + c4

                    def ev_o(tl, ps, co=co):
                        xs = XT[:, co, t0 + tl * 512: t0 + (tl + 1) * 512]
                        tt(xs, xs, ps[:, :], ALU.add)
                    dense_ws(sl_o, c4 * 128, 8, lambda kc, tl: mgh[:, kc, tsl(tl)], range(2), ev_o)
        if dbg and dbg[0] == f"xm{l}":
            dump(XT[:, 2, :], 2048)

        if STOP < 8:
            continue
        norm_stats(88 * KB)
        h2T = r3(VB(0, 8 * T, BF), 8)
        norm_apply(h2T, PC_N2, 0, T)
        rT = r3(VB(40 * KB, 8 * T, BF), 8)
        f_t = [VB(72 * KB + i * 2 * KB, 512) for i in range(2)]
        fctr = [0]
        for g in range(4):
            for cg in range(2):
                sl_1 = slab(w_ff1_d[l][:, g * 1024 + cg * 512: g * 1024 + (cg + 1) * 512], 8, 512)
                for c4 in range(4):
                    hc = cg * 4 + c4

                    def ev_f1(t, ps, hc=hc):
                        ft = f_t[fctr[0] % 2]
                        fctr[0] += 1
                        act(ft, ps[:, :], AF.Relu)
                        tt(rT[:, hc, tsl(t)], ft, ft, ALU.mult)
                    dense_ws(sl_1, c4 * 128, 8, lambda kc, t: h2T[:, kc, tsl(t)], range(NTT), ev_f1)
            for cg in range(2):
                sl_2 = slab(w_ff2_d[l][g * 1024:(g + 1) * 1024, cg * 512:(cg + 1) * 512], 8, 512)
                for c4 in range(4):
                    co = cg * 4 + c4

                    def ev_f2(t, ps, co=co):
                        tt(XT[:, co, tsl(t)], XT[:, co, tsl(t)], ps[:, :], ALU.add)
                    dense_ws(sl_2, c4 * 128, 8, lambda kc, t: rT[:, kc, tsl(t)], range(NTT), ev_f2)

    out_toks = []
    for tb in range(16):
        xo = VB(56 * KB + (tb % 4) * 4 * KB, 1024)
        for c4 in range(2):
            ps = bank()
            for j in range(4):
                c = c4 * 4 + j
                tr(ps[:, j * 128:(j + 1) * 128], XT[:, c, tb * 128:(tb + 1) * 128], inc=(j == 3))
            act(xo[:, c4 * 512:(c4 + 1) * 512], ps[:, :], AF.Copy)
        out_toks.append(dma("sp", out_d[tb * 128:(tb + 1) * 128, :], xo, reads=[xo], writes=[("dram", "out", tb)]))
    final_waits = {}
    for s, v in out_toks:
        final_waits[s] = max(final_waits.get(s, 0), v)
    if dbg_d is not None:
        t = S.lastw.get(("dram", "dbg"))
        if t:
            final_waits[t[0]] = max(final_waits.get(t[0], 0), t[1])

    with es:
        with nc.Block() as block:
            def run(name, e):
                for waits, fn, incspec in S.ops[name]:
                    for s, v in waits:
                        e.wait_ge(sems[s], v)
                    ins = fn(e)
                    if incspec is not None:
                        ins.then_inc(sems[incspec[0]], incspec[1])
                if name == "sp":
                    for s, v in final_waits.items():
                        e.wait_ge(sems[s], v)

            @block.tensor
            def _(e):
                run("pe", e)

            @block.scalar
            def _(e):
                run("act", e)

            @block.vector
            def _(e):
                run("dve", e)

            @block.gpsimd
            def _(e):
                run("pool", e)

            @block.sync
            def _(e):
                run("sp", e)
    return nc


def _host_consts(hf):
    cstf = np.zeros((128, CF_N), np.float32)
    cstf[:, CF_ID:CF_ID + 128] = np.eye(128, dtype=np.float32)
    s_idx = np.arange(128)[:, None]
    t_idx = np.arange(128)[None, :]
    cstf[:, CF_TRI:CF_TRI + 128] = (t_idx >= s_idx).astype(np.float32)
    kl = np.arange(128, dtype=np.float64)
    for h in range(4):
        for d in range(16):
            dlt = d * 128 - 1536
            cstf[:, CF_BOWN + h * 16 + d] = SLOPES[h] * (kl + dlt - 256)
        for d in range(28):
            dlt = d * 128 - 3584
            cstf[:, CF_BPRE + h * 28 + d] = SLOPES[h] * (kl + dlt - 256) if hf == 1 else NEG
    cstf[:, CF_HF] = float(hf)
    cstb = np.zeros((128, CB_N), np.float32)
    cstb[:, CB_ONES:CB_ONES + 128] = 1.0
    p = np.arange(128)
    cstb[:, CB_BLK:CB_BLK + 128] = (p[:, None] // 64 == p[None, :] // 64).astype(np.float32)
    cstb[:, CB_ID:CB_ID + 128] = np.eye(128)
    k_ = np.arange(128)[:, None]
    q_ = np.arange(128)[None, :]
    for h in range(4):
        c = np.zeros((128, 128))
        same = (k_ // 64) == (q_ // 64)
        c = np.where((k_ > q_) & same, -16.0 * SLOPES[h] * (k_ - q_), 0.0)
        c = np.where((k_ // 64) > (q_ // 64), 8.0 * NEG, c)
        cstb[:, CB_CORR + h * 128: CB_CORR + (h + 1) * 128] = c
    return cstf, cstb.astype(ml_dtypes.bfloat16)


def _host_params(norm1_w, norm2_w, gate_b, b_qnorm_w, b_knorm_w, c_conv_w, a_vnorm_w, b_subnorm_w, b_lam, a_bs):
    pcol = np.zeros((128, 2 * PC_L), np.float32)
    pbc = np.zeros((128, 2 * PB_L), np.float32)
    for l in range(2):
        o = l * PC_L
        pcol[:, o + PC_N1:o + PC_N1 + 8] = norm1_w[l].reshape(8, 128).T
        pcol[:, o + PC_N2:o + PC_N2 + 8] = norm2_w[l].reshape(8, 128).T
        pcol[:, o + PC_GB:o + PC_GB + 24] = gate_b[l].reshape(24, 128).T
        pcol[:, o + PC_QN] = b_qnorm_w[l].reshape(128)
        pcol[:, o + PC_KN] = b_knorm_w[l].reshape(128)
        pcol[:, o + PC_CW:o + PC_CW + 12] = c_conv_w[l].reshape(12, 128).T
        pcol[:, o + PC_SUB] = b_subnorm_w[l]
        o = l * PB_L
        pbc[:, o + PB_VN:o + PB_VN + 512] = a_vnorm_w[l][None, :]
        pbc[:, o + PB_SUB:o + PB_SUB + 128] = b_subnorm_w[l][None, :]
        pbc[:, o + PB_LAM:o + PB_LAM + 256] = b_lam[l].reshape(256)[None, :]
        pbc[:, o + PB_BS:o + PB_BS + 512] = a_bs[l].reshape(512)[None, :]
    return pcol, pbc


_NC_CACHE = {}


def kernel(x, norm1_w, w_in, gate_b, a_vnorm_w, a_ws, a_bs, b_qnorm_w, b_knorm_w, b_lam,
           b_subnorm_w, c_conv_w, w_br_a, w_br_b, w_br_c, w_out, norm2_w, w_ff1, w_ff2, _dbg=None, _layers=(0, 1)):
    f = lambda a: np.ascontiguousarray(np.asarray(a, dtype=np.float32))
    x = f(x)
    key = (tuple(_layers), _dbg)
    if key not in _NC_CACHE:
        _NC_CACHE[key] = build(list(_layers), True, True, dbg=_dbg)
    nc = _NC_CACHE[key]
    pcol, pbc = _host_params(f(norm1_w), f(norm2_w), f(gate_b), f(b_qnorm_w), f(b_knorm_w), f(c_conv_w),
                             f(a_vnorm_w), f(b_subnorm_w), f(b_lam), f(a_bs))
    w_br = np.ascontiguousarray(np.stack([f(w_br_a), f(w_br_b), f(w_br_c)], axis=1))
    wsT = np.ascontiguousarray(np.transpose(f(a_ws), (0, 3, 1, 2)))
    shared = {"w_in": f(w_in), "w_br": w_br, "w_out": f(w_out), "w_ff1": f(w_ff1), "w_ff2": f(w_ff2),
              "wsT": wsT, "pcol": pcol, "pbc": pbc}
    if os.environ.get("K_LIGHT"):
        for k_ in ("w_in", "w_br", "w_out", "w_ff1", "w_ff2"):
            shared[k_] = np.zeros((2, 8, 8), np.float32)
    in_maps = []
    for c in range(8):
        b, hf = c // 2, c % 2
        cstf, cstb = _host_consts(hf)
        m = dict(shared)
        m["x"] = np.ascontiguousarray(x[b, hf * T:(hf + 1) * T, :])
        m["cstf"] = cstf
        m["cstb"] = cstb
        in_maps.append(m)
    res = run_bass_kernel_spmd(nc, in_maps, core_ids=list(range(8)))
    out = np.empty((4, 4096, D), np.float32)
    for c in range(8):
        b, hf = c // 2, c % 2
        out[b, hf * T:(hf + 1) * T, :] = np.asarray(res.results[c]["out"], dtype=np.float32)
    if _dbg:
        return out, [np.asarray(res.results[c]["dbg"], dtype=np.float32) for c in range(8)]
    return out
```
